# Optimizing a Trainium2 kernel written in Bass

```python
import jax, jax.numpy as jnp
from jax import lax
import numpy as np

D_MODEL = 2048
BATCH = 2
SEQ = 4096
DEPTH = 1

CHUNK = 64
EPS = 1e-6
M_HEADS = 4
M_DV = 256
M_DQK = 128
M_WIDTH = M_HEADS * M_DV
M_QK_WIDTH = M_HEADS * M_DQK
C_WIDTH = 1024
C_GROUPS = 8
C_KERNEL = 3
D_FF = 5504
F_KERNEL = 3
SPLIT_SIZES = (M_QK_WIDTH, M_QK_WIDTH, M_WIDTH, M_WIDTH, M_HEADS, M_HEADS,
               C_WIDTH, C_WIDTH, C_WIDTH, D_MODEL, D_MODEL)
IN_WIDTH = sum(SPLIT_SIZES)
F_GATE_OFFSET = 2 * M_QK_WIDTH + 2 * M_WIDTH + M_HEADS

kernel_name = "hybrid_mlstm_shortconv_gated_block"


def rmsnorm(x, g):
    x32 = x.astype(jnp.float32)
    y = x32 * lax.rsqrt(jnp.mean(x32 * x32, axis=-1, keepdims=True) + EPS)
    return (y * g.astype(jnp.float32)).astype(x.dtype)


def causal_dwconv(x, w, b=None):
    K = w.shape[0]
    T = x.shape[1]
    xp = jnp.pad(x, ((0, 0), (K - 1, 0), (0, 0)))
    y = xp[:, 0:T, :] * w[0]
    for j in range(1, K):
        y = y + xp[:, j:j + T, :] * w[j]
    if b is not None:
        y = y + b
    return y


def mlstm_chunkwise(q, k, v, i_pre, f_pre):
    B, T, H, DQK = q.shape
    DV = v.shape[-1]
    NC = T // CHUNK
    L = CHUNK
    f32 = jnp.float32

    def to_chunks(a):
        a = jnp.moveaxis(a.astype(f32), 2, 1)
        return a.reshape((B, H, NC, L) + a.shape[3:])

    qc = to_chunks(q)
    kc = to_chunks(k) * (DQK ** -0.5)
    vc = to_chunks(v)
    logi = to_chunks(i_pre)
    logf = jax.nn.log_sigmoid(to_chunks(f_pre))

    b = jnp.cumsum(logf, axis=-1)
    a = b[..., -1]
    w = a[..., None] - b + logi
    m_loc = jnp.max(w, axis=-1)
    p = jnp.exp(w - m_loc[..., None])
    C_loc = jnp.einsum('bhcs,bhcsv,bhcsk->bhcvk', p, vc, kc)
    n_loc = jnp.einsum('bhcs,bhcsk->bhck', p, kc)

    def step(carry, xs):
        C, n, m = carry
        a_c, ml, Cl, nl = xs
        m_new = jnp.maximum(a_c + m, ml)
        s_old = jnp.exp(a_c + m - m_new)
        s_new = jnp.exp(ml - m_new)
        C_new = s_old[..., None, None] * C + s_new[..., None, None] * Cl
        n_new = s_old[..., None] * n + s_new[..., None] * nl
        return (C_new, n_new, m_new), (C, n, m)

    init = (jnp.zeros((B, H, DV, DQK), f32), jnp.zeros((B, H, DQK), f32), jnp.zeros((B, H), f32))
    xs = (jnp.moveaxis(a, 2, 0), jnp.moveaxis(m_loc, 2, 0),
          jnp.moveaxis(C_loc, 2, 0), jnp.moveaxis(n_loc, 2, 0))
    _, (C_prev, n_prev, m_prev) = lax.scan(step, init, xs)
    C_prev = jnp.moveaxis(C_prev, 0, 2)
    n_prev = jnp.moveaxis(n_prev, 0, 2)
    m_prev = jnp.moveaxis(m_prev, 0, 2)

    causal = jnp.tril(jnp.ones((L, L), dtype=bool))
    Dm = b[..., :, None] - b[..., None, :] + logi[..., None, :]
    Dm = jnp.where(causal, Dm, -jnp.inf)
    m_inter = b + m_prev[..., None]
    m_t = jnp.maximum(jnp.max(Dm, axis=-1), m_inter)
    S = jnp.einsum('bhctk,bhcsk->bhcts', qc, kc) * jnp.exp(Dm - m_t[..., None])
    inter_scale = jnp.exp(m_inter - m_t)
    num = (jnp.einsum('bhcts,bhcsv->bhctv', S, vc)
           + inter_scale[..., None] * jnp.einsum('bhcvk,bhctk->bhctv', C_prev, qc))
    den = jnp.sum(S, axis=-1) + inter_scale * jnp.einsum('bhck,bhctk->bhct', n_prev, qc)
    h = num / jnp.maximum(jnp.abs(den), jnp.exp(-m_t))[..., None]
    h = h.reshape(B, H, T, DV)
    return jnp.moveaxis(h, 1, 2)


def setup_inputs(seed: int = 0) -> dict:
    key = jax.random.key(seed)
    ks = jax.random.split(key, 16)
    f32 = jnp.float32
    nrm = lambda k, shape, scale: (jax.random.normal(k, shape, f32) * scale)
    x = jax.random.normal(ks[0], (BATCH, SEQ, D_MODEL), f32)
    norm_mix_g = 1.0 + nrm(ks[1], (D_MODEL,), 0.02)
    w_in = nrm(ks[2], (D_MODEL, IN_WIDTH), D_MODEL ** -0.5)
    b_in = nrm(ks[3], (IN_WIDTH,), 0.02)
    b_in = b_in.at[F_GATE_OFFSET:F_GATE_OFFSET + M_HEADS].add(jnp.linspace(3.0, 6.0, M_HEADS))
    mlstm_head_g = 1.0 + nrm(ks[4], (M_WIDTH,), 0.02)
    w_branch_m = nrm(ks[5], (M_WIDTH, D_MODEL), M_WIDTH ** -0.5)
    conv_w = nrm(ks[6], (C_KERNEL, C_WIDTH), C_KERNEL ** -0.5)
    w_branch_c = nrm(ks[7], (C_WIDTH, D_MODEL), C_WIDTH ** -0.5)
    w_out = nrm(ks[8], (D_MODEL, D_MODEL), D_MODEL ** -0.5)
    norm_ffn_g = 1.0 + nrm(ks[9], (D_MODEL,), 0.02)
    w_up = nrm(ks[10], (D_MODEL, 2 * D_FF), D_MODEL ** -0.5)
    ffn_conv_w = nrm(ks[11], (F_KERNEL, 2 * D_FF), F_KERNEL ** -0.5)
    ffn_conv_b = nrm(ks[12], (2 * D_FF,), 0.02)
    w_down = nrm(ks[13], (D_FF, D_MODEL), D_FF ** -0.5)
    norm_out_g = 1.0 + nrm(ks[14], (D_MODEL,), 0.02)
    return {"x": x, "norm_mix_g": norm_mix_g, "w_in": w_in, "b_in": b_in,
            "mlstm_head_g": mlstm_head_g, "w_branch_m": w_branch_m, "conv_w": conv_w,
            "w_branch_c": w_branch_c, "w_out": w_out, "norm_ffn_g": norm_ffn_g,
            "w_up": w_up, "ffn_conv_w": ffn_conv_w, "ffn_conv_b": ffn_conv_b,
            "w_down": w_down, "norm_out_g": norm_out_g}


def reference(x, norm_mix_g, w_in, b_in, mlstm_head_g, w_branch_m, conv_w, w_branch_c,
              w_out, norm_ffn_g, w_up, ffn_conv_w, ffn_conv_b, w_down, norm_out_g):
    B, T, _ = x.shape
    h = x
    for _layer in range(DEPTH):
        hn = rmsnorm(h, norm_mix_g)
        proj = hn @ w_in + b_in
        offs = [int(o) for o in np.cumsum(SPLIT_SIZES)[:-1]]
        (q, k, v, o_pre, i_pre, f_pre, cb, cc, ch, gm_pre, gc_pre) = jnp.split(proj, offs, axis=-1)

        hm = mlstm_chunkwise(q.reshape(B, T, M_HEADS, M_DQK), k.reshape(B, T, M_HEADS, M_DQK),
                             v.reshape(B, T, M_HEADS, M_DV), i_pre, f_pre)
        hm = hm * lax.rsqrt(jnp.mean(hm * hm, axis=-1, keepdims=True) + EPS)
        hm = hm.reshape(B, T, M_WIDTH) * mlstm_head_g.astype(jnp.float32)
        hm = hm.astype(x.dtype) * jax.nn.sigmoid(o_pre)
        y_m = hm @ w_branch_m

        y_c = (cb * causal_dwconv(cc * ch, conv_w)) @ w_branch_c

        merged = jax.nn.sigmoid(gm_pre) * y_m + jax.nn.sigmoid(gc_pre) * y_c
        h = h + merged @ w_out

        hn = rmsnorm(h, norm_ffn_g)
        u = causal_dwconv(hn @ w_up, ffn_conv_w, ffn_conv_b)
        gate, val = jnp.split(u, 2, axis=-1)
        h = h + (jax.nn.silu(gate) * val) @ w_down
    return rmsnorm(h, norm_out_g)
```

```python
import numpy as np
import concourse.bass as bass
import concourse.mybir as mybir
from concourse.bass_utils import run_bass_kernel_spmd

F32 = mybir.dt.float32
BF16 = mybir.dt.bfloat16
AF = mybir.ActivationFunctionType
ALU = mybir.AluOpType

D = 2048
NPRE = 24
NT = NPRE + 9
LT = 1088
DFF = 5504
NFF = 43
EPS = 1e-6
BLK = 256
import os
STOP_AFTER = os.environ.get("STOP_AFTER") or None
ENGS = ("pe", "act", "dve", "pool", "sp")


class V:
    __slots__ = ("ap", "regs")

    def __init__(self, ap, regs):
        self.ap = ap
        self.regs = regs


def DR(ap):
    return V(ap, [])


class Buf:
    def __init__(self, space, flat_ap, lo, shape, es):
        self.space, self.lo, self.shape, self.es = space, lo, tuple(shape), es
        n = int(np.prod(shape))
        self.nbytes = n * es
        if len(shape) == 1:
            self.ap = flat_ap
        elif len(shape) == 2:
            self.ap = flat_ap.rearrange("p (a b) -> p a b", a=shape[0], b=shape[1])
        else:
            self.ap = flat_ap.rearrange("p (a b c) -> p a b c", a=shape[0], b=shape[1], c=shape[2])
        st = []
        acc = 1
        for s in reversed(shape):
            st.append(acc)
            acc *= s
        self.strides = tuple(reversed(st))

    def v(self, *idx, p=None):
        idx = list(idx) + [slice(None)] * (len(self.shape) - len(idx))
        mn = mx = 0
        for i, s, st in zip(idx, self.shape, self.strides):
            if isinstance(i, int):
                a, b = i, i
            else:
                a = 0 if i.start is None else i.start
                b = (s if i.stop is None else i.stop) - 1
            mn += a * st
            mx += b * st
        ps = slice(None) if p is None else slice(p[0], p[1])
        ap = self.ap[(ps,) + tuple(idx)]
        return V(ap, [(self.space, self.lo + mn * self.es, self.lo + (mx + 1) * self.es)])


class Arena:
    def __init__(self, space, tensor, total):
        self.space, self.t, self.total, self.ptr = space, tensor, total, 0
        self.peak = 0

    def alloc(self, shape, dtype):
        es = 4 if dtype is F32 else 2
        n = int(np.prod(shape))
        lo = (self.ptr + 63) // 64 * 64
        nb = n * es
        assert lo + nb <= self.total, (self.space, lo, nb, self.total)
        self.ptr = lo + nb
        self.peak = max(self.peak, self.ptr)
        flat = self.t[:, lo // 2:(lo + nb) // 2]
        if dtype is F32:
            flat = flat.bitcast(F32)
        return Buf(self.space, flat, lo, shape, es)

    def at(self, lo, shape, dtype):
        save = self.ptr
        self.ptr = lo
        b = self.alloc(shape, dtype)
        self.ptr = save
        return b


class Op:
    __slots__ = ("eng", "fn", "dma", "deps", "needed", "seq", "sem", "val", "idx")


class Prog:
    def __init__(self, nc, n_dma_sems=20):
        self.nc = nc
        self.q = {e: [] for e in ENGS}
        self.trk = {"sb": {}, "ps": {}}
        self.n_dma_sems = n_dma_sems
        self.dma_uses = [0] * n_dma_sems
        self.dma_rr = 0
        self.nops = 0

    @staticmethod
    def _blocks(v):
        for (space, lo, hi) in v.regs:
            blk = 2048 if space == "ps" else BLK
            for b in range(lo // blk, (hi - 1) // blk + 1):
                yield space, b

    def op(self, eng, fn, reads=(), writes=(), dma=False):
        o = Op()
        o.eng, o.fn, o.dma, o.needed, o.seq, o.idx = eng, fn, dma, False, 0, self.nops
        self.nops += 1
        if dma:
            k = self.dma_rr % self.n_dma_sems
            self.dma_rr += 1
            self.dma_uses[k] += 1
            o.sem, o.val = k, 16 * self.dma_uses[k]
        raw, other = set(), set()
        ps_reads = [v for v in reads if v.regs and v.regs[0][0] == "ps"]
        for v in ps_reads:
            for space, b in self._blocks(v):
                st = self.trk[space].get(b)
                if st is not None and st[0] is not None:
                    raw.add(st[0])
        writes = list(writes) + ps_reads
        reads = [v for v in reads if not (v.regs and v.regs[0][0] == "ps")]
        for v in reads:
            for space, b in self._blocks(v):
                st = self.trk[space].get(b)
                if st is not None and st[0] is not None:
                    raw.add(st[0])
        for v in writes:
            for space, b in self._blocks(v):
                st = self.trk[space].get(b)
                if st is not None:
                    if st[0] is not None:
                        other.add(st[0])
                    other.update(st[1].values())
        for v in writes:
            for space, b in self._blocks(v):
                self.trk[space][b] = [o, {}]
        for v in reads:
            for space, b in self._blocks(v):
                st = self.trk[space].get(b)
                if st is None:
                    st = self.trk[space][b] = [None, {}]
                st[1][("d", o.idx) if dma else eng] = o
        deps = set()
        for d in raw | other:
            if d is o:
                continue
            if (not d.dma) and (not dma) and d.eng == eng:
                if eng == "pe" or d not in raw:
                    continue
            deps.add(d)
        for d in deps:
            d.needed = True
        o.deps = deps
        self.q[eng].append(o)
        return o

    def fence(self, eng, ops):
        o = Op()
        o.eng, o.fn, o.dma, o.needed, o.seq, o.idx = eng, None, False, False, 0, self.nops
        self.nops += 1
        o.deps = set(ops)
        for d in ops:
            d.needed = True
        self.q[eng].append(o)

    def emit(self, block, esem, dsem):
        for e in ENGS:
            c = 0
            for o in self.q[e]:
                if o.needed and not o.dma:
                    c += 1
                    o.seq = c
        handles = {"pe": block.tensor, "act": block.scalar, "dve": block.vector,
                   "pool": block.gpsimd, "sp": block.sync}

        def run(e, eng):
            waited_e = {p: 0 for p in ENGS}
            waited_d = [0] * self.n_dma_sems
            for o in self.q[e]:
                need_e = {}
                need_d = {}
                for d in o.deps:
                    if d.dma:
                        need_d[d.sem] = max(need_d.get(d.sem, 0), d.val)
                    else:
                        need_e[d.eng] = max(need_e.get(d.eng, 0), d.seq)
                for p, s in need_e.items():
                    if s > waited_e[p]:
                        eng.wait_ge(esem[p], s)
                        waited_e[p] = s
                for k, val in need_d.items():
                    if val > waited_d[k]:
                        eng.wait_ge(dsem[k], val)
                        waited_d[k] = val
                if o.fn is None:
                    continue
                ins = o.fn(eng)
                if o.dma:
                    ins.then_inc(dsem[o.sem], 16)
                elif o.needed:
                    ins.then_inc(esem[e], 1)

        for e in ENGS:
            handles[e](lambda eng, e=e: run(e, eng))


def build_program():
    nc = bass.Bass("TRN2", target_bir_lowering=False)
    dram = {}

    def din(name, shape):
        dram[name] = nc.dram_tensor(name, list(shape), F32, kind="ExternalInput").ap()
        return dram[name]

    xall = din("xall", [NPRE * 128 + LT, D])
    maskd = din("mask", [128, NT + 1])
    constd = din("consts", [128, 3, 128])
    vecsd = din("vecs", [488, 128])
    browd = din("brow", [1, 3080])
    g3d = din("g3", [D])
    w_in = din("w_in", [D, 10248])
    FULL = STOP_AFTER in (None, "GO", "C", "G", "O", "F0", "F1")
    if FULL:
        w_bm = din("w_branch_m", [1024, D])
        w_bc = din("w_branch_c", [1024, D])
        w_out = din("w_out", [D, D])
        w_up = din("w_up", [D, 2 * DFF])
        w_down = din("w_down", [DFF, D])
    nc._in_names = None
    yout = nc.dram_tensor("y", [1024, D], F32, kind="ExternalOutput").ap()
    dbg = None
    if STOP_AFTER is not None:
        dbg = nc.dram_tensor("dbg", [128, 9 * 1024], F32, kind="ExternalOutput").ap()

    w_in_v = w_in.rearrange("(kc p) n -> p kc n", p=128)
    if FULL:
        w_bm_v = w_bm.rearrange("(kc p) n -> p kc n", p=128)
        w_bc_v = w_bc.rearrange("(kc p) n -> p kc n", p=128)
        w_out_v = w_out.rearrange("(kc p) n -> p kc n", p=128)
        w_up_v = w_up.rearrange("(kc p) n -> p kc n", p=128)
        w_down_v = w_down.rearrange("(kc p) n -> p kc n", p=128)
    nc._in_names = set(dram.keys())

    SB_TOTAL = 206 * 1024
    from contextlib import ExitStack
    with ExitStack() as es:
        arena_t = es.enter_context(nc.sbuf_tensor("arena", [128, SB_TOTAL // 2], BF16))
        ps_t = es.enter_context(nc.psum_tensor("ps", [128, 4096], F32))
        esem = {e: es.enter_context(nc.semaphore("es_" + e)) for e in ENGS}
        PG = Prog(nc)
        dsem = [es.enter_context(nc.semaphore("ds%d" % i)) for i in range(PG.n_dma_sems)]
        block = es.enter_context(nc.Block())

        SB = Arena("sb", arena_t, SB_TOTAL)
        psf = ps_t[:, :]
        psb = psf.bitcast(BF16)

        def PS(lo, shape, dtype=F32):
            es_ = 4 if dtype is F32 else 2
            n = int(np.prod(shape))
            if dtype is F32:
                flat = psf[:, lo // 4:lo // 4 + n]
            else:
                flat = psb[:, lo // 2:lo // 2 + n]
            return Buf("ps", flat, lo, shape, es_)

        BANK = 2048

        def mm(out, lhsT, rhs, start=True, stop=True):
            PG.op("pe", lambda e: e.matmul(out.ap, lhsT=lhsT.ap, rhs=rhs.ap, start=start, stop=stop),
                  reads=[lhsT, rhs], writes=[out])

        def tr(out, in_, ident):
            PG.op("pe", lambda e: e.transpose(out=out.ap, in_=in_.ap, identity=ident.ap),
                  reads=[in_, ident], writes=[out])

        def act(out, in_, func, scale=None, bias=None, accum=None):
            reads = [in_]
            kw = {}
            if scale is not None:
                if isinstance(scale, V):
                    reads.append(scale)
                    kw["scale"] = scale.ap
                else:
                    kw["scale"] = float(scale)
            if bias is not None:
                if isinstance(bias, V):
                    reads.append(bias)
                    kw["bias"] = bias.ap
                else:
                    kw["bias"] = float(bias)
            writes = [out]
            if accum is not None:
                writes.append(accum)
                kw["accum_out"] = accum.ap
            PG.op("act", lambda e: e.activation(out=out.ap, in_=in_.ap, func=func, **kw),
                  reads=reads, writes=writes)

        def ts(eng, out, in0, s1, op0, s2=None, op1=None):
            reads = [in0]
            a1 = s1
            if isinstance(s1, V):
                reads.append(s1)
                a1 = s1.ap
            a2 = s2
            if isinstance(s2, V):
                reads.append(s2)
                a2 = s2.ap
            kw = {}
            if op1 is not None:
                kw["op1"] = op1
            PG.op(eng, lambda e: e.tensor_scalar(out=out.ap, in0=in0.ap, scalar1=a1, scalar2=a2, op0=op0, **kw),
                  reads=reads, writes=[out])

        def tt(eng, out, in0, in1, op):
            PG.op(eng, lambda e: e.tensor_tensor(out=out.ap, in0=in0.ap, in1=in1.ap, op=op),
                  reads=[in0, in1], writes=[out])

        def stt(out, in0, scalar, in1, op0, op1):
            reads = [in0, in1]
            sc = scalar
            if isinstance(scalar, V):
                reads.append(scalar)
                sc = scalar.ap
            PG.op("dve", lambda e: e.scalar_tensor_tensor(out=out.ap, in0=in0.ap, scalar=sc, in1=in1.ap, op0=op0, op1=op1),
                  reads=reads, writes=[out])

        def cp(eng, out, in_):
            if eng == "act":
                PG.op("act", lambda e: e.copy(out=out.ap, in_=in_.ap), reads=[in_], writes=[out])
            else:
                PG.op(eng, lambda e: e.tensor_copy(out=out.ap, in_=in_.ap), reads=[in_], writes=[out])

        def recip(out, in_):
            PG.op("dve", lambda e: e.reciprocal(out=out.ap, in_=in_.ap), reads=[in_], writes=[out])

        def memset(eng, out, val):
            PG.op(eng, lambda e: e.memset(out.ap, val), reads=[], writes=[out])

        def dma(eng, out, in_):
            return PG.op(eng, lambda e: e.dma_start(out=out.ap, in_=in_.ap), reads=[in_], writes=[out], dma=True)

        def bc(v, shape):
            return V(v.ap.to_broadcast(list(shape)), v.regs)

        cf = SB.alloc([3, 128], F32)
        identb = SB.alloc([128], BF16)
        Ub = SB.alloc([128], BF16)
        colv = SB.alloc([488], F32)
        browb = SB.alloc([3080], BF16)
        onesrow = SB.alloc([128], BF16)
        maskb = SB.alloc([NT + 1], F32)
        ss = SB.alloc([NT], F32)
        rs = SB.alloc([NT], F32)
        rstd = SB.alloc([NT], F32)
        ea = SB.alloc([NT, 4], F32)
        junk = SB.alloc([2048], BF16)
        gt = [dict((nm, SB.alloc([4], F32)) for nm in ("e", "l", "logf", "tmp", "wcol", "eb", "den", "d", "rd", "r1", "ssq", "sq2", "rstd2", "sc", "res"))
              for _ in range(2)]
        if_sb = [SB.alloc([8], F32) for _ in range(2)]
        glo = [dict((nm, SB.alloc([4], BF16)) for nm in ("hi", "lo")) for _ in range(2)]
        onesb = SB.alloc([128], BF16)
        wcb = [SB.alloc([4, 2], BF16) for _ in range(2)]
        stage = SB.alloc([128], F32)

        epsc = SB.alloc([1], F32)
        onec = SB.alloc([1], F32)
        hnT = SB.alloc([16, LT], BF16)
        hm_pre = SB.alloc([9, 1024], BF16)
        MARK1 = SB.ptr

        ident_f = cf.v(0)
        U_f = cf.v(1)
        ones_f = cf.v(2)

        dma("sp", cf.v(), DR(constd[:, :, :]))
        dma("sp", maskb.v(), DR(maskd[:, :]))
        dma("pool", browb.v(p=(0, 1)), DR(browd[:, :]))
        cp("dve", identb.v(), ident_f)
        cp("dve", Ub.v(), U_f)
        cp("dve", onesb.v(), ones_f)
        memset("dve", onesrow.v(p=(0, 1)), 1.0)
        pst = PS(7 * BANK, [128], F32)
        for r0 in range(0, 488, 128):
            n = min(128, 488 - r0)
            dma("sp", stage.v(p=(0, n)), DR(vecsd[r0:r0 + n, :]))
            PG.op("pe", lambda e, n=n: e.transpose(out=pst.ap[:, 0:n], in_=stage.ap[0:n, :], identity=cf.ap[0:n, 0, 0:n]),
                  reads=[stage.v(p=(0, n)), ident_f], writes=[pst.v(slice(0, n))])
            cp("dve", colv.v(slice(r0, r0 + n)), pst.v(slice(0, n)))
        C_BQ, C_BK, C_BV, C_BO, C_BCB, C_BCC, C_BCH, C_BGM, C_BGC = 0, 4, 8, 16, 24, 32, 40, 48, 64
        C_G1, C_GH, C_CW, C_G2, C_FW, C_FB = 80, 96, 104, 128, 144, 402

        Wres = SB.alloc([16, 2056], BF16)
        xt = [SB.alloc([D], F32) for _ in range(2)]
        hn_tok = [SB.alloc([D], BF16) for _ in range(2)]
        hnTt = [SB.alloc([16, 128], BF16) for _ in range(2)]
        k_tok = [SB.alloc([512], BF16) for _ in range(2)]
        q_tok = [SB.alloc([512], BF16) for _ in range(2)]
        vw = [SB.alloc([4, 256], BF16) for _ in range(2)]
        qkT = SB.alloc([8, 128], BF16)
        ST = SB.alloc([4, 128], BF16)
        Gv = SB.alloc([4, 256], F32)
        Gn = SB.alloc([4, 2], F32)
        Cv = SB.alloc([4, 256], BF16)
        Cn = SB.alloc([4, 2], BF16)

        for (c0, c1, s0) in [(512, 768, 512), (768, 1024, 768), (1024, 1280, 1024), (1280, 1536, 1280),
                             (1536, 1792, 1536), (1792, 2048, 1792), (0, 256, 0), (256, 512, 256)]:
            dma("pool", Wres.v(slice(None), slice(c0, c1)), DR(w_in_v[:, :, s0:s0 + 256]))
        dma("pool", Wres.v(slice(None), slice(2048, 2056)), DR(w_in_v[:, :, 3072:3080]))

        memset("dve", Gv.v(), 0.0)
        memset("dve", Gn.v(), 0.0)
        memset("dve", Cv.v(), 0.0)
        memset("dve", Cn.v(), 0.0)

        psT = PS(0, [16, 128], BF16)
        ps_k = PS(2 * BANK, [512], F32)
        ps_q = PS(3 * BANK, [512], F32)
        ps_v = PS(4 * BANK, [1024], F32)
        ps_v4 = PS(4 * BANK, [4, 256], F32)
        hsc = SB.alloc([4, 256], F32)
        sqb = SB.alloc([4, 256], F32)
        ps_if = PS(6 * BANK, [8], F32)
        ps_b = PS(6 * BANK + 64, [4], F32)
        ps_a = PS(6 * BANK + 128, [4], F32)
        ps_den = PS(6 * BANK + 192, [4, 2], F32)
        ps_kvn = PS(6 * BANK + 256, [4, 2], F32)
        ps_qkT = PS(7 * BANK, [8, 128], BF16)
        ps_qk = PS(2 * BANK, [4, 128], F32)
        ps_num = PS(4 * BANK, [4, 256], F32)
        ps_kv01 = PS(3 * BANK, [2, 256], F32)
        ps_kv23 = PS(7 * BANK, [2, 256], F32)

        def tile_rows(ti):
            if ti < NPRE:
                return ti * 128, 128
            li = ti - NPRE
            if li == 0:
                return NPRE * 128, 64
            return NPRE * 128 + 64 + (li - 1) * 128, 128

        def lcol(li):
            return (0, 64) if li == 0 else (64 + (li - 1) * 128, 128)

        def front(ti):
            r0, P = tile_rows(ti)
            par = ti % 2
            pp = (0, P)
            dma("sp", xt[par].v(p=pp), DR(xall[r0:r0 + P, :]))
            act(junk.v(p=pp), xt[par].v(p=pp), AF.Square, accum=ss.v(slice(ti, ti + 1), p=pp))
            act(rs.v(slice(ti, ti + 1), p=pp), ss.v(slice(ti, ti + 1), p=pp), AF.Sqrt, scale=1.0 / D, bias=epsc.v(p=pp))
            recip(rstd.v(slice(ti, ti + 1), p=pp), rs.v(slice(ti, ti + 1), p=pp))
            ts("dve", hn_tok[par].v(p=pp), xt[par].v(p=pp), rstd.v(slice(ti, ti + 1), p=pp), ALU.mult)
            for kc in range(16):
                tr(psT.v(kc, slice(0, P)), hn_tok[par].v(slice(kc * 128, (kc + 1) * 128), p=pp),
                   V(identb.ap[0:P, 0:P], identb.v().regs))
            g1b = bc(V(colv.ap[:, C_G1:C_G1 + 16].unsqueeze(2), colv.v(slice(C_G1, C_G1 + 16)).regs), [128, 16, P])
            if ti >= NPRE:
                c0, _ = lcol(ti - NPRE)
                dst = hnT.v(slice(None), slice(c0, c0 + P))
            else:
                dst = hnTt[par].v(slice(None), slice(0, P))
            tt("dve", dst, psT.v(slice(None), slice(0, P)), g1b, ALU.mult)

        def hsrc(ti, kc, P):
            if ti >= NPRE:
                c0, _ = lcol(ti - NPRE)
                return hnT.v(kc, slice(c0, c0 + P))
            return hnTt[ti % 2].v(kc, slice(0, P))

        def proj_tok(ti, P, out_ps, wc0, wc1, b0):
            n = wc1 - wc0
            mm(V(out_ps.ap[0:P], out_ps.regs), onesrow.v(slice(0, P), p=(0, 1)), browb.v(slice(b0, b0 + n), p=(0, 1)),
               start=True, stop=False)
            for kc in range(16):
                mm(V(out_ps.ap[0:P], out_ps.regs), hsrc(ti, kc, P), Wres.v(kc, slice(wc0, wc1)),
                   start=False, stop=(kc == 15))

        def back(ti):
            r0, P = tile_rows(ti)
            par = ti % 2
            pp = (0, P)
            local = ti >= NPRE
            li = ti - NPRE
            g = gt[par]
            proj_tok(ti, P, ps_if.v(), 2048, 2056, 2048)
            proj_tok(ti, P, ps_k.v(), 512, 1024, 512)
            proj_tok(ti, P, ps_v.v(slice(0, 512)), 1024, 1536, 1024)
            proj_tok(ti, P, ps_v.v(slice(512, 1024)), 1536, 2048, 1536)
            if local:
                proj_tok(ti, P, ps_q.v(), 0, 512, 0)
            BSTEP = int(os.environ.get("BSTEP", "99"))
            if BSTEP < 1:
                return
            act(k_tok[par].v(p=pp), ps_k.v(p=pp), AF.Copy, scale=float(128 ** -0.5))
            if local:
                act(q_tok[par].v(p=pp), ps_q.v(p=pp), AF.Copy)
            cp("dve", if_sb[par].v(p=pp), ps_if.v(p=pp))
            if BSTEP < 2:
                return
            act(g["e"].v(p=pp), if_sb[par].v(slice(4, 8), p=pp), AF.Exp, scale=-1.0)
            act(g["l"].v(p=pp), g["e"].v(p=pp), AF.Ln, bias=onec.v(p=pp))
            ts("dve", g["logf"].v(p=pp), g["l"].v(p=pp), -1.0, ALU.mult)
            cp("dve", glo[par]["hi"].v(p=pp), g["logf"].v(p=pp))
            tt("dve", g["res"].v(p=pp), g["logf"].v(p=pp), glo[par]["hi"].v(p=pp), ALU.subtract)
            cp("dve", glo[par]["lo"].v(p=pp), g["res"].v(p=pp))
            Uv = V(Ub.ap[0:P, 0:P], Ub.v().regs)
            Ov = V(onesb.ap[0:P, :], onesb.v().regs)
            mm(ps_b.v(p=pp), Uv, glo[par]["hi"].v(p=pp), start=True, stop=False)
            mm(ps_b.v(p=pp), Uv, glo[par]["lo"].v(p=pp), start=False, stop=True)
            mm(ps_a.v(), Ov, glo[par]["hi"].v(p=pp), start=True, stop=False)
            mm(ps_a.v(), Ov, glo[par]["lo"].v(p=pp), start=False, stop=True)
            tt("dve", g["tmp"].v(p=pp), if_sb[par].v(slice(0, 4), p=pp), ps_b.v(p=pp), ALU.subtract)
            act(g["wcol"].v(p=pp), g["tmp"].v(p=pp), AF.Exp)
            ts("dve", g["wcol"].v(p=pp), g["wcol"].v(p=pp), maskb.v(slice(ti, ti + 1), p=pp), ALU.mult)
            act(ea.v(ti), ps_a.v(), AF.Exp)
            if local:
                act(g["eb"].v(p=pp), ps_b.v(p=pp), AF.Exp)
            if BSTEP < 3:
                return
            def bcl(buf, n):
                return V(buf.ap[0:P].unsqueeze(2).to_broadcast([P, 4, n]), buf.v(p=pp).regs)
            tt("dve", vw[par].v(p=pp), ps_v4.v(p=pp), bcl(g["wcol"], 256), ALU.mult)
            cp("dve", wcb[par].v(p=pp), bcl(g["wcol"], 2))
            if local:
                for h in range(4):
                    tr(ps_qkT.v(h, slice(0, P)), k_tok[par].v(slice(h * 128, (h + 1) * 128), p=pp),
                       V(identb.ap[0:P, 0:P], identb.v().regs))
                    tr(ps_qkT.v(4 + h, slice(0, P)), q_tok[par].v(slice(h * 128, (h + 1) * 128), p=pp),
                       V(identb.ap[0:P, 0:P], identb.v().regs))
                cp("act", qkT.v(slice(None), slice(0, P)), ps_qkT.v(slice(None), slice(0, P)))
                for h in range(4):
                    mm(ps_qk.v(h, slice(0, P), p=pp), qkT.v(h, slice(0, P)), qkT.v(4 + h, slice(0, P)))
                Ubb = V(Ub.ap[0:P, 0:P].unsqueeze(1).to_broadcast([P, 4, P]), Ub.v().regs)
                tt("dve", ST.v(slice(None), slice(0, P), p=pp), ps_qk.v(slice(None), slice(0, P), p=pp), Ubb, ALU.mult)
                for h in range(4):
                    mm(ps_num.v(h, p=pp), ST.v(h, slice(0, P), p=pp), vw[par].v(h, p=pp), start=True, stop=False)
                    mm(ps_num.v(h, p=pp), qkT.v(4 + h, slice(0, P)), Cv.v(h), start=False, stop=True)
                    mm(ps_den.v(h, p=pp), ST.v(h, slice(0, P), p=pp), wcb[par].v(h, p=pp),
                       start=True, stop=False)
                    mm(ps_den.v(h, p=pp), qkT.v(4 + h, slice(0, P)), Cn.v(h),
                       start=False, stop=True)
                tt("dve", g["den"].v(p=pp), ps_den.v(slice(None), 0, p=pp), g["eb"].v(p=pp), ALU.mult)
                act(g["d"].v(p=pp), g["den"].v(p=pp), AF.Abs)
                ts("dve", g["d"].v(p=pp), g["d"].v(p=pp), 1.0, ALU.max)
                recip(g["rd"].v(p=pp), g["d"].v(p=pp))
                tt("dve", g["r1"].v(p=pp), g["rd"].v(p=pp), g["eb"].v(p=pp), ALU.mult)
                tt("dve", hsc.v(p=pp), ps_num.v(p=pp), bcl(g["r1"], 256), ALU.mult)
                tt("dve", sqb.v(p=pp), hsc.v(p=pp), hsc.v(p=pp), ALU.mult)
                PG.op("dve", lambda e, P=P, g=g: e.tensor_reduce(out=g["ssq"].ap[0:P], in_=sqb.ap[0:P], axis=mybir.AxisListType.X, op=ALU.add),
                      reads=[sqb.v(p=pp)], writes=[g["ssq"].v(p=pp)])
                act(g["sq2"].v(p=pp), g["ssq"].v(p=pp), AF.Sqrt, scale=1.0 / 256, bias=epsc.v(p=pp))
                recip(g["rstd2"].v(p=pp), g["sq2"].v(p=pp))
                hmv = V(hm_pre.ap[0:P, li].rearrange("p (h v) -> p h v", h=4), hm_pre.v(li, p=pp).regs)
                tt("dve", hmv, hsc.v(p=pp), bcl(g["rstd2"], 256), ALU.mult)
            if BSTEP < 4:
                return
            for h in range(4):
                kvps = (ps_kv01 if h < 2 else ps_kv23).v(h % 2)
                mm(kvps, k_tok[par].v(slice(h * 128, (h + 1) * 128), p=pp), vw[par].v(h, p=pp))
                mm(ps_kvn.v(h), k_tok[par].v(slice(h * 128, (h + 1) * 128), p=pp),
                   wcb[par].v(h, p=pp))
            for h in range(4):
                kvps = (ps_kv01 if h < 2 else ps_kv23).v(h % 2)
                if ti == 0:
                    cp("dve", Gv.v(h), kvps)
                else:
                    stt(Gv.v(h), Gv.v(h), ea.v(ti - 1, slice(h, h + 1)), kvps, ALU.mult, ALU.add)
            if ti == 0:
                cp("dve", Gn.v(), ps_kvn.v())
            else:
                tt("dve", Gn.v(), Gn.v(), V(ea.ap[:, ti - 1].unsqueeze(2).to_broadcast([128, 4, 2]), ea.v(ti - 1).regs), ALU.mult)
                tt("dve", Gn.v(), Gn.v(), ps_kvn.v(), ALU.add)
            if ti < NT - 1:
                for h in range(4):
                    ts("dve", Cv.v(h), Gv.v(h), ea.v(ti, slice(h, h + 1)), ALU.mult)
                tt("dve", Cn.v(), Gn.v(), V(ea.ap[:, ti].unsqueeze(2).to_broadcast([128, 4, 2]), ea.v(ti).regs), ALU.mult)

        memset("dve", epsc.v(), EPS)
        memset("dve", onec.v(), 1.0)

        def finish_dbg(src, n):
            o_ = dma("pool", DR(dbg[:, 0:n]), src)
            PG.fence("pool", [o_])
            PG.emit(block, esem, dsem)

        if STOP_AFTER == "S":
            finish_dbg(colv.v(), 488)
            return nc
        if STOP_AFTER == "F":
            for ti in range(NPRE, NT):
                front(ti)
            finish_dbg(V(hnT.ap.rearrange("p a b -> p (a b)")[:, 0:9216], hnT.v().regs), 9216)
            return nc
        if STOP_AFTER == "B0":
            front(0)
            back(0)
            finish_dbg(V(Gv.ap.rearrange("p a b -> p (a b)"), Gv.v().regs), 1024)
            return nc
        if STOP_AFTER == "B1":
            front(NPRE + 1)
            back(NPRE + 1)
            finish_dbg(V(hm_pre.ap.rearrange("p a b -> p (a b)"), hm_pre.v().regs), 9216)
            return nc
        tsel = os.environ.get("TSEL")
        tlist = list(range(NT)) if not tsel else list(range(int(tsel.split(":")[0]), int(tsel.split(":")[1])))
        front(tlist[0])
        for n_, ti in enumerate(tlist):
            if n_ + 1 < len(tlist):
                front(tlist[n_ + 1])
            back(ti)

        def flat2(buf):
            return V(buf.ap.rearrange("p a b -> p (a b)"), buf.v().regs)
        if STOP_AFTER == "P":
            finish_dbg(flat2(hm_pre), 9216)
            return nc


        SB.ptr = MARK1
        merged = SB.alloc([16, LT], BF16)
        R_MERGED = merged.lo
        NWB = 3
        WB = [SB.alloc([16, 256], BF16) for _ in range(NWB)]
        wbi = [0]
        MARK2 = SB.ptr

        def wload(view, c0, ncols, nk=16, k0=0):
            b = WB[wbi[0] % NWB]
            wbi[0] += 1
            dst = V(b.ap[:, 0:nk, 0:ncols], b.v(slice(0, nk), slice(0, ncols)).regs)
            dma("pool", dst, DR(view[:, k0:k0 + nk, c0:c0 + ncols]))
            return b

        PB = [PS(i * BANK, [512], F32) for i in range(8)]
        SPL = [(0, 363), (363, 726), (726, 1088)]
        SPF = [(62, 404), (404, 746), (746, 1088)]
        flagc = maskb.v(slice(NT, NT + 1))

        def colc(c):
            return colv.v(slice(c, c + 1))

        def proj_feat(wt, srcbuf, nk, banks, splits):
            for si, (s0, s1) in enumerate(splits):
                for kc in range(nk):
                    mm(PB[banks[si]].v(slice(0, s1 - s0)), wt.v(kc, slice(0, 128)), srcbuf.v(kc, slice(s0, s1)),
                       start=(kc == 0), stop=(kc == nk - 1))

        hmT = SB.alloc([8, LT], BF16)
        MARK3 = SB.ptr
        Wo = SB.alloc([16, 1024], BF16)
        sgo = SB.alloc([1024], BF16)
        hm_tok = SB.alloc([1024], BF16)
        for c in range(4):
            dma("pool", Wo.v(slice(None), slice(c * 256, (c + 1) * 256)), DR(w_in_v[:, :, 2048 + c * 256:2048 + (c + 1) * 256]))
        psTT = PS(0, [8, 128], BF16)
        for li in range(9):
            c0, P = lcol(li)
            pp = (0, P)
            for g2 in range(2):
                ob = PB[2 + g2]
                mm(ob.v(p=pp), onesrow.v(slice(0, P), p=(0, 1)), browb.v(slice(2056 + g2 * 512, 2056 + (g2 + 1) * 512), p=(0, 1)),
                   start=True, stop=False)
                for kc in range(16):
                    mm(ob.v(p=pp), hnT.v(kc, slice(c0, c0 + P)), Wo.v(kc, slice(g2 * 512, (g2 + 1) * 512)),
                       start=False, stop=(kc == 15))
                act(sgo.v(slice(g2 * 512, (g2 + 1) * 512), p=pp), ob.v(p=pp), AF.Sigmoid)
            tt("dve", hm_tok.v(p=pp), hm_pre.v(li, p=pp), sgo.v(p=pp), ALU.mult)
            for kc in range(8):
                tr(psTT.v(kc, slice(0, P)), hm_tok.v(slice(kc * 128, (kc + 1) * 128), p=pp),
                   V(identb.ap[0:P, 0:P], identb.v().regs))
            ghb = bc(V(colv.ap[:, C_GH:C_GH + 8].unsqueeze(2), colv.v(slice(C_GH, C_GH + 8)).regs), [128, 8, P])
            tt("dve", hmT.v(slice(None), slice(c0, c0 + P)), psTT.v(slice(None), slice(0, P)), ghb, ALU.mult)

        if STOP_AFTER == "GO":
            finish_dbg(V(flat2(hmT).ap[:, 0:8 * LT], hmT.v().regs), 8 * LT)
            return nc
        SB.ptr = MARK3
        ycin = SB.alloc([8, LT], BF16)
        MARK4 = SB.ptr
        ccs = SB.alloc([LT], F32)
        zb = SB.alloc([LT], F32)
        yv = SB.alloc([LT], F32)
        memset("dve", yv.v(slice(0, 2)), 0.0)
        for m in range(8):
            wt = wload(w_in_v, 4104 + 128 * m, 128)
            proj_feat(wt, hnT, 16, (2, 3, 4), SPL)
            for si, (s0, s1) in enumerate(SPL):
                ts("dve", ccs.v(slice(s0, s1)), PB[2 + si].v(slice(0, s1 - s0)), colc(C_BCC + m), ALU.add)
            wt = wload(w_in_v, 5128 + 128 * m, 128)
            proj_feat(wt, hnT, 16, (5, 6, 7), SPL)
            for si, (s0, s1) in enumerate(SPL):
                stt(zb.v(slice(s0, s1)), PB[5 + si].v(slice(0, s1 - s0)), colc(C_BCH + m), ccs.v(slice(s0, s1)), ALU.add, ALU.mult)
            ts("dve", zb.v(slice(0, 64)), zb.v(slice(0, 64)), flagc, ALU.mult)
            ts("dve", yv.v(slice(2, LT)), zb.v(slice(2, LT)), colc(C_CW + 16 + m), ALU.mult)
            stt(yv.v(slice(2, LT)), zb.v(slice(1, LT - 1)), colc(C_CW + 8 + m), yv.v(slice(2, LT)), ALU.mult, ALU.add)
            stt(yv.v(slice(2, LT)), zb.v(slice(0, LT - 2)), colc(C_CW + m), yv.v(slice(2, LT)), ALU.mult, ALU.add)
            wt = wload(w_in_v, 3080 + 128 * m, 128)
            proj_feat(wt, hnT, 16, (2, 3, 4), SPL)
            for si, (s0, s1) in enumerate(SPL):
                stt(ycin.v(m, slice(s0, s1)), PB[2 + si].v(slice(0, s1 - s0)), colc(C_BCB + m), yv.v(slice(s0, s1)), ALU.add, ALU.mult)

        if STOP_AFTER == "C":
            finish_dbg(V(flat2(ycin).ap[:, 0:8 * LT], ycin.v().regs), 8 * LT)
            return nc
        SB.ptr = MARK4
        sA = SB.alloc([LT], F32)
        sB = SB.alloc([LT], F32)
        tA = SB.alloc([LT], F32)
        tB = SB.alloc([LT], F32)
        for m in range(16):
            wt = wload(w_in_v, 6152 + 128 * m, 128)
            proj_feat(wt, hnT, 16, (2, 3, 4), SPL)
            for si, (s0, s1) in enumerate(SPL):
                ts("dve", sA.v(slice(s0, s1)), PB[2 + si].v(slice(0, s1 - s0)), colc(C_BGM + m), ALU.add)
            act(sA.v(), sA.v(), AF.Sigmoid)
            wt = wload(w_bm_v, 128 * m, 128, nk=8)
            proj_feat(wt, hmT, 8, (5, 6, 7), SPL)
            for si, (s0, s1) in enumerate(SPL):
                tt("dve", tA.v(slice(s0, s1)), PB[5 + si].v(slice(0, s1 - s0)), sA.v(slice(s0, s1)), ALU.mult)
            wt = wload(w_in_v, 8200 + 128 * m, 128)
            proj_feat(wt, hnT, 16, (2, 3, 4), SPL)
            for si, (s0, s1) in enumerate(SPL):
                ts("dve", sB.v(slice(s0, s1)), PB[2 + si].v(slice(0, s1 - s0)), colc(C_BGC + m), ALU.add)
            act(sB.v(), sB.v(), AF.Sigmoid)
            wt = wload(w_bc_v, 128 * m, 128, nk=8)
            proj_feat(wt, ycin, 8, (5, 6, 7), SPL)
            for si, (s0, s1) in enumerate(SPL):
                tt("dve", tB.v(slice(s0, s1)), PB[5 + si].v(slice(0, s1 - s0)), sB.v(slice(s0, s1)), ALU.mult)
            tt("dve", merged.v(m), tA.v(), tB.v(), ALU.add)

        if STOP_AFTER == "G":
            finish_dbg(V(flat2(merged).ap[:, 0:9216], merged.v().regs), 9216)
            return nc
        SB.ptr = MARK2
        h1 = SB.alloc([9, D], F32)
        for li in range(9):
            r0, P = tile_rows(NPRE + li)
            dma("sp", h1.v(li, p=(0, P)), DR(xall[r0:r0 + P, :]))
        cnt = 0
        for cg in range(8):
            wt = wload(w_out_v, cg * 256, 256)
            for li in range(9):
                c0, P = lcol(li)
                pp = (0, P)
                ob = PB[cnt % 2]
                cnt += 1
                for kc in range(16):
                    mm(ob.v(slice(0, 256), p=pp), merged.v(kc, slice(c0, c0 + P)), wt.v(kc, slice(0, 256)),
                       start=(kc == 0), stop=(kc == 15))
                tt("dve", h1.v(li, slice(cg * 256, (cg + 1) * 256), p=pp), ob.v(slice(0, 256), p=pp),
                   h1.v(li, slice(cg * 256, (cg + 1) * 256), p=pp), ALU.add)

        if STOP_AFTER == "O":
            finish_dbg(V(flat2(h1).ap[:, 0:9216], h1.v().regs), 9216)
            return nc
        actb = [SB.at(R_MERGED, [9, 1024], BF16), SB.at(hm_pre.lo, [9, 1024], BF16)]
        fsc = R_MERGED + 9 * 1024 * 2
        tg = SB.at(fsc, [1024], F32)
        sgf = SB.at(fsc + 4096, [1024], F32)
        tv = SB.at(fsc + 8192, [1024], F32)
        hn2_tok = SB.at(fsc + 12288, [D], BF16)
        for li in range(9):
            c0, P = lcol(li)
            pp = (0, P)
            col = slice(li, li + 1)
            act(junk.v(p=pp), h1.v(li, p=pp), AF.Square, accum=ss.v(col, p=pp))
            act(rs.v(col, p=pp), ss.v(col, p=pp), AF.Sqrt, scale=1.0 / D, bias=epsc.v(p=pp))
            recip(rstd.v(col, p=pp), rs.v(col, p=pp))
            if li == 0:
                ts("dve", rstd.v(col, p=pp), rstd.v(col, p=pp), maskb.v(slice(NPRE, NPRE + 1), p=pp), ALU.mult)
            ts("dve", hn2_tok.v(p=pp), h1.v(li, p=pp), rstd.v(col, p=pp), ALU.mult)
            for kc in range(16):
                tr(psT.v(kc, slice(0, P)), hn2_tok.v(slice(kc * 128, (kc + 1) * 128), p=pp),
                   V(identb.ap[0:P, 0:P], identb.v().regs))
            g2b = bc(V(colv.ap[:, C_G2:C_G2 + 16].unsqueeze(2), colv.v(slice(C_G2, C_G2 + 16)).regs), [128, 16, P])
            tt("dve", hnT.v(slice(None), slice(c0, c0 + P)), psT.v(slice(None), slice(0, P)), g2b, ALU.mult)

        if STOP_AFTER == "F0":
            finish_dbg(V(flat2(hnT).ap[:, 0:9216], hnT.v().regs), 9216)
            return nc
        def taps(banks, dst, c):
            for j in (2, 1, 0):
                for si in range(3):
                    lo = max(0, 342 * si - j)
                    hi = min(1024, 342 * si + 342 - j)
                    src = PB[banks[si]].v(slice(lo + j - 342 * si, hi + j - 342 * si))
                    wj = colc(C_FW + j * 86 + c)
                    if j == 2:
                        ts("dve", dst.v(slice(lo, hi)), src, wj, ALU.mult, colc(C_FB + c), ALU.add)
                    else:
                        stt(dst.v(slice(lo, hi)), src, wj, dst.v(slice(lo, hi)), ALU.mult, ALU.add)

        blocks = [(0, 9), (9, 9), (18, 9), (27, 8), (35, 8)]
        if os.environ.get("SKIP_F1"):
            blocks = []
        cnt = 0
        for bi, (f0, nfl) in enumerate(blocks):
            ab = actb[bi % 2]
            for fl in range(nfl):
                f = f0 + fl
                wt = wload(w_up_v, 128 * f, 128)
                proj_feat(wt, hnT, 16, (2, 3, 4), SPF)
                taps((2, 3, 4), tg, f)
                act(sgf.v(), tg.v(), AF.Silu)
                wt = wload(w_up_v, DFF + 128 * f, 128)
                proj_feat(wt, hnT, 16, (5, 6, 7), SPF)
                taps((5, 6, 7), tv, 43 + f)
                tt("dve", ab.v(fl), sgf.v(), tv.v(), ALU.mult)
            for cg in range(8):
                wt = wload(w_down_v, cg * 256, 256, nk=nfl, k0=f0)
                for li in range(1, 9):
                    ob = PB[cnt % 2]
                    cnt += 1
                    for fl in range(nfl):
                        mm(ob.v(slice(0, 256)), ab.v(fl, slice((li - 1) * 128, li * 128)), wt.v(fl, slice(0, 256)),
                           start=(fl == 0), stop=(fl == nfl - 1))
                    tt("dve", h1.v(li, slice(cg * 256, (cg + 1) * 256)), ob.v(slice(0, 256)),
                       h1.v(li, slice(cg * 256, (cg + 1) * 256)), ALU.add)

        if STOP_AFTER == "F1":
            finish_dbg(V(flat2(h1).ap[:, 0:9216], h1.v().regs), 9216)
            return nc
        g3b = SB.at(R_MERGED, [D], F32)
        ot = [SB.at(R_MERGED + 8192, [D], F32), SB.at(R_MERGED + 16384, [D], F32)]
        dma("sp", g3b.v(), DR(g3d.partition_broadcast(128)))
        outs = []
        for li in range(1, 9):
            col = slice(9 + li, 10 + li)
            act(junk.v(), h1.v(li), AF.Square, accum=ss.v(col))
            act(rs.v(col), ss.v(col), AF.Sqrt, scale=1.0 / D, bias=epsc.v())
            recip(rstd.v(col), rs.v(col))
            o_t = ot[li % 2]
            stt(o_t.v(), h1.v(li), rstd.v(col), g3b.v(), ALU.mult, ALU.mult)
            outs.append(dma("sp", DR(yout[(li - 1) * 128:li * 128, :]), o_t.v()))
        PG.fence("sp", outs)
        PG.emit(block, esem, dsem)
    return nc


_CACHE = {}


def _host_consts():
    ident = np.eye(128, dtype=np.float32)
    U = np.triu(np.ones((128, 128), dtype=np.float32))
    ones = np.ones((128, 128), dtype=np.float32)
    return np.ascontiguousarray(np.stack([ident, U, ones], axis=1))


def kernel(x, norm_mix_g, w_in, b_in, mlstm_head_g, w_branch_m, conv_w, w_branch_c, w_out, norm_ffn_g,
           w_up, ffn_conv_w, ffn_conv_b, w_down, norm_out_g):
    f32 = np.float32
    x = np.asarray(x, f32)
    b_in = np.asarray(b_in, f32)
    if "nc" not in _CACHE:
        _CACHE["nc"] = build_program()
    nc = _CACHE["nc"]
    vecs = np.concatenate([
        b_in[0:3072].reshape(24, 128), b_in[3080:10248].reshape(56, 128),
        np.asarray(norm_mix_g, f32).reshape(16, 128), np.asarray(mlstm_head_g, f32).reshape(8, 128),
        np.asarray(conv_w, f32).reshape(24, 128), np.asarray(norm_ffn_g, f32).reshape(16, 128),
        np.asarray(ffn_conv_w, f32).reshape(258, 128), np.asarray(ffn_conv_b, f32).reshape(86, 128)], axis=0)
    brow = np.concatenate([b_in[0:2048], b_in[3072:3080], b_in[2048:3072]]).reshape(1, 3080)
    consts = _host_consts()
    shared = {"consts": consts, "vecs": np.ascontiguousarray(vecs), "brow": np.ascontiguousarray(brow),
              "g3": np.asarray(norm_out_g, f32), "w_in": np.asarray(w_in, f32),
              "w_branch_m": np.asarray(w_branch_m, f32), "w_branch_c": np.asarray(w_branch_c, f32),
              "w_out": np.asarray(w_out, f32), "w_up": np.asarray(w_up, f32), "w_down": np.asarray(w_down, f32)}
    in_maps = []
    NR = NPRE * 128 + LT
    shared = {k: v for k, v in shared.items() if k in nc._in_names}
    ranks = [int(c) for c in os.environ["DBG_RANKS"].split(",")] if os.environ.get("DBG_RANKS") else list(range(8))
    for r in ranks:
        b, j = r // 4, r % 4
        g0 = 1024 * j - 64 - NPRE * 128
        xa = np.zeros((NR, D), f32)
        lo = max(0, -g0)
        xa[lo:] = x[b, g0 + lo:g0 + NR]
        mask = np.zeros((128, NT + 1), f32)
        mask[:, NT] = 1.0 if j > 0 else 0.0
        for ti in range(NT):
            if ti < NPRE:
                r0, P = ti * 128, 128
            elif ti == NPRE:
                r0, P = NPRE * 128, 64
            else:
                r0, P = NPRE * 128 + 64 + (ti - NPRE - 1) * 128, 128
            gi = g0 + r0 + np.arange(P)
            mask[:P, ti] = (gi >= 0).astype(f32)
        m = dict(shared)
        m["xall"] = xa
        m["mask"] = mask
        in_maps.append(m)
    res = run_bass_kernel_spmd(nc, in_maps, core_ids=list(range(len(ranks))))
    if STOP_AFTER is not None:
        return {r: res.results[i]["dbg"] for i, r in enumerate(ranks)}
    out = np.zeros((2, 4096, D), f32)
    for i, r in enumerate(ranks):
        b, j = r // 4, r % 4
        out[b, 1024 * j:1024 * (j + 1)] = res.results[i]["y"]
    return out
```
